# Optimizing a Trainium2 kernel written in Bass

```python
import math
import jax, jax.numpy as jnp
from jax import lax
import numpy as np

D_MODEL = 2048
BATCH = 1
SEQ = 16384
DEPTH = 2

SSM_WIDTH = D_MODEL // 2
SSM_GROUP = 16
SSM_GROUPS = SSM_WIDTH // SSM_GROUP
SSM_STATE = 64
HEAD_DIM = 64
ATT_HEADS = D_MODEL // (2 * HEAD_DIM)
ATT_KV_HEADS = ATT_HEADS // 4
ATT_GROUP = ATT_HEADS // ATT_KV_HEADS
ATT_Q_WIDTH = ATT_HEADS * HEAD_DIM
ATT_KV_WIDTH = ATT_KV_HEADS * HEAD_DIM
WINDOW = 128
ATT_BLOCK = 128
ROPE_THETA = 10000.0
GLA_HEADS = 4
GLA_V_WIDTH = D_MODEL // 2
GLA_DV = GLA_V_WIDTH // GLA_HEADS
GLA_DK = GLA_DV // 2
GLA_QK_WIDTH = GLA_HEADS * GLA_DK
GLA_GATE_RANK = 16
GLA_TAU = 16.0
GLA_CHUNK = 64
D_FF = 4 * D_MODEL
EPS = 1e-6
IN_WIDTH = (SSM_WIDTH + ATT_Q_WIDTH + 2 * ATT_KV_WIDTH + 2 * GLA_QK_WIDTH + GLA_V_WIDTH
            + GLA_GATE_RANK + GLA_V_WIDTH + 3 * D_MODEL)

kernel_name = "hybrid_s5_swa_gla_gated_block"


def _split_points():
    sizes = [SSM_WIDTH, ATT_Q_WIDTH, ATT_KV_WIDTH, ATT_KV_WIDTH, GLA_QK_WIDTH, GLA_QK_WIDTH,
             GLA_V_WIDTH, GLA_GATE_RANK, GLA_V_WIDTH, 3 * D_MODEL]
    return [int(v) for v in np.cumsum(sizes)[:-1]]


def rms_norm(x, g):
    xf = x.astype(jnp.float32)
    y = xf * lax.rsqrt(jnp.mean(xf * xf, axis=-1, keepdims=True) + EPS)
    return (y * g.astype(jnp.float32)).astype(x.dtype)


def rope(x, pos):
    half = HEAD_DIM // 2
    inv = ROPE_THETA ** (-jnp.arange(half, dtype=jnp.float32) / half)
    ang = pos.astype(jnp.float32)[:, None] * inv[None, :]
    cos = jnp.cos(ang)[None, :, None, :]
    sin = jnp.sin(ang)[None, :, None, :]
    xf = x.astype(jnp.float32)
    x1, x2 = xf[..., :half], xf[..., half:]
    return jnp.concatenate([x1 * cos - x2 * sin, x2 * cos + x1 * sin], axis=-1).astype(x.dtype)


def _cplx_combine(e1, e2):
    a1r, a1i, b1r, b1i = e1
    a2r, a2i, b2r, b2i = e2
    return (a2r * a1r - a2i * a1i,
            a2r * a1i + a2i * a1r,
            a2r * b1r - a2i * b1i + b2r,
            a2r * b1i + a2i * b1r + b2i)


def s5_branch(u, lam_re, lam_im, log_step, b_re, b_im, c_re, c_im, d_skip, w_glu, b_glu):
    bsz, s_len, _ = u.shape
    uf = u.astype(jnp.float32)
    ug = uf.reshape(bsz, s_len, SSM_GROUPS, SSM_GROUP)
    step = jnp.exp(log_step.astype(jnp.float32))[:, None]
    lr = lam_re.astype(jnp.float32)
    li = lam_im.astype(jnp.float32)
    mag = jnp.exp(lr * step)
    ar = mag * jnp.cos(li * step)
    ai = mag * jnp.sin(li * step)
    den = lr * lr + li * li
    fr = ((ar - 1.0) * lr + ai * li) / den
    fi = (ai * lr - (ar - 1.0) * li) / den
    br = b_re.astype(jnp.float32)
    bi = b_im.astype(jnp.float32)
    bbr = fr[..., None] * br - fi[..., None] * bi
    bbi = fr[..., None] * bi + fi[..., None] * br
    xr = jnp.einsum('bsgc,gpc->sbgp', ug, bbr)
    xi = jnp.einsum('bsgc,gpc->sbgp', ug, bbi)
    a_r = jnp.broadcast_to(ar[None, None], xr.shape)
    a_i = jnp.broadcast_to(ai[None, None], xr.shape)
    _, _, hr, hi = lax.associative_scan(_cplx_combine, (a_r, a_i, xr, xi), axis=0)
    y = (jnp.einsum('sbgp,gcp->bsgc', hr, c_re.astype(jnp.float32))
         - jnp.einsum('sbgp,gcp->bsgc', hi, c_im.astype(jnp.float32)))
    y = y.reshape(bsz, s_len, SSM_WIDTH) + d_skip.astype(jnp.float32) * uf
    z = jax.nn.gelu(y)
    z = z * jax.nn.sigmoid(z @ w_glu.astype(jnp.float32) + b_glu.astype(jnp.float32))
    return z.astype(u.dtype)


def swa_branch(q, k, v, sinks, pos):
    bsz, s_len = q.shape[:2]
    nb = s_len // ATT_BLOCK
    q = rope(q, pos)
    k = rope(k, pos)
    qb = q.reshape(bsz, nb, ATT_BLOCK, ATT_KV_HEADS, ATT_GROUP, HEAD_DIM)
    pad = ((0, 0), (ATT_BLOCK, 0), (0, 0), (0, 0))
    kp = jnp.pad(k, pad).reshape(bsz, nb + 1, ATT_BLOCK, ATT_KV_HEADS, HEAD_DIM)
    vp = jnp.pad(v, pad).reshape(bsz, nb + 1, ATT_BLOCK, ATT_KV_HEADS, HEAD_DIM)
    kb = jnp.concatenate([kp[:, :-1], kp[:, 1:]], axis=2)
    vb = jnp.concatenate([vp[:, :-1], vp[:, 1:]], axis=2)
    scores = jnp.einsum('bnqhgd,bnkhd->bnhgqk', qb, kb).astype(jnp.float32) * (HEAD_DIM ** -0.5)
    qi = jnp.arange(ATT_BLOCK)[:, None]
    kj = jnp.arange(2 * ATT_BLOCK)[None, :]
    diff = qi + ATT_BLOCK - kj
    band = (diff >= 0) & (diff < WINDOW)
    key_pos = jnp.arange(nb)[:, None] * ATT_BLOCK - ATT_BLOCK + jnp.arange(2 * ATT_BLOCK)[None, :]
    valid = band[None] & (key_pos >= 0)[:, None, :]
    scores = jnp.where(valid[None, :, None, None], scores, -jnp.inf)
    sink = sinks.astype(jnp.float32).reshape(ATT_KV_HEADS, ATT_GROUP)[None, None, :, :, None, None]
    sink = jnp.broadcast_to(sink, scores.shape[:-1] + (1,))
    probs = jax.nn.softmax(jnp.concatenate([scores, sink], axis=-1), axis=-1)[..., :-1]
    o = jnp.einsum('bnhgqk,bnkhd->bnqhgd', probs.astype(v.dtype), vb)
    return o.reshape(bsz, s_len, ATT_Q_WIDTH)


def gla_branch(q, k, v, gate_lr, out_gate, w_gate_up, b_gate, norm_g):
    bsz, s_len = q.shape[:2]
    nc = s_len // GLA_CHUNK
    f32 = jnp.float32
    log_a = jax.nn.log_sigmoid(gate_lr.astype(f32) @ w_gate_up.astype(f32) + b_gate.astype(f32)) / GLA_TAU
    shp_k = (bsz, nc, GLA_CHUNK, GLA_HEADS, GLA_DK)
    shp_v = (bsz, nc, GLA_CHUNK, GLA_HEADS, GLA_DV)
    qc = q.astype(f32).reshape(shp_k) * (GLA_DK ** -0.5)
    kc = k.astype(f32).reshape(shp_k)
    vc = v.astype(f32).reshape(shp_v)
    bcum = jnp.cumsum(log_a.reshape(shp_k), axis=2)
    b_last = bcum[:, :, -1]
    q_in = qc * jnp.exp(bcum)
    k_in = kc * jnp.exp(-bcum)
    att = jnp.einsum('bnihd,bnjhd->bnhij', q_in, k_in)
    causal = jnp.tril(jnp.ones((GLA_CHUNK, GLA_CHUNK), dtype=bool))
    att = jnp.where(causal, att, 0.0)
    o_intra = jnp.einsum('bnhij,bnjhe->bnihe', att, vc)
    k_dec = kc * jnp.exp(b_last[:, :, None] - bcum)
    upd = jnp.einsum('bnjhd,bnjhe->bnhde', k_dec, vc)
    decay = jnp.exp(b_last)

    def chunk_step(state, inp):
        u_n, g_n = inp
        return g_n[..., None] * state + u_n, state

    init = jnp.zeros((bsz, GLA_HEADS, GLA_DK, GLA_DV), f32)
    _, starts = lax.scan(chunk_step, init, (jnp.moveaxis(upd, 1, 0), jnp.moveaxis(decay, 1, 0)))
    o_inter = jnp.einsum('bnihd,nbhde->bnihe', q_in, starts)
    o = (o_intra + o_inter).reshape(bsz, s_len, GLA_HEADS, GLA_DV)
    o = o * lax.rsqrt(jnp.mean(o * o, axis=-1, keepdims=True) + EPS)
    o = o.reshape(bsz, s_len, GLA_V_WIDTH) * norm_g.astype(f32)
    o = o * jax.nn.silu(out_gate.astype(f32))
    return o.astype(q.dtype)


def setup_inputs(seed: int = 0) -> dict:
    key = jax.random.key(seed)
    ks = jax.random.split(key, 32)
    L, D = DEPTH, D_MODEL
    G, P, C = SSM_GROUPS, SSM_STATE, SSM_GROUP

    def nrm(k, shape, scale):
        return jax.random.normal(k, shape, jnp.float32) * scale

    x = nrm(ks[0], (BATCH, SEQ, D), 1.0)
    norm1_g = 1.0 + nrm(ks[1], (L, D), 0.02)
    w_in = nrm(ks[2], (L, D, IN_WIDTH), D ** -0.5)
    ssm_lam_re = -0.5 + nrm(ks[3], (L, G, P), 0.01)
    ssm_lam_im = math.pi * jnp.arange(P, dtype=jnp.float32)[None, None, :] + nrm(ks[4], (L, G, P), 0.01)
    ssm_log_step = jax.random.uniform(ks[5], (L, G), jnp.float32, math.log(1e-3), math.log(1e-1))
    ssm_b_re = nrm(ks[6], (L, G, P, C), (2.0 * C) ** -0.5)
    ssm_b_im = nrm(ks[7], (L, G, P, C), (2.0 * C) ** -0.5)
    ssm_c_re = nrm(ks[8], (L, G, C, P), (2.0 * P) ** -0.5)
    ssm_c_im = nrm(ks[9], (L, G, C, P), (2.0 * P) ** -0.5)
    ssm_d = nrm(ks[10], (L, SSM_WIDTH), 1.0)
    ssm_w_glu = nrm(ks[11], (L, SSM_WIDTH, SSM_WIDTH), SSM_WIDTH ** -0.5)
    ssm_b_glu = nrm(ks[12], (L, SSM_WIDTH), 0.02)
    att_sinks = nrm(ks[13], (L, ATT_HEADS), 0.5)
    gla_w_gate = nrm(ks[14], (L, GLA_GATE_RANK, GLA_QK_WIDTH), GLA_GATE_RANK ** -0.5)
    gla_b_gate = nrm(ks[15], (L, GLA_QK_WIDTH), 0.1)
    gla_norm_g = 1.0 + nrm(ks[16], (L, GLA_V_WIDTH), 0.02)
    w_branch_ssm = nrm(ks[17], (L, SSM_WIDTH, D), SSM_WIDTH ** -0.5)
    w_branch_att = nrm(ks[18], (L, ATT_Q_WIDTH, D), ATT_Q_WIDTH ** -0.5)
    w_branch_gla = nrm(ks[19], (L, GLA_V_WIDTH, D), GLA_V_WIDTH ** -0.5)
    w_out = nrm(ks[20], (L, D, D), D ** -0.5)
    norm2_g = 1.0 + nrm(ks[21], (L, D), 0.02)
    w_ff1 = nrm(ks[22], (L, D, D_FF), D ** -0.5)
    w_ff2 = nrm(ks[23], (L, D_FF, D), D_FF ** -0.5)
    final_norm_g = 1.0 + nrm(ks[24], (D,), 0.02)
    return {"x": x, "norm1_g": norm1_g, "w_in": w_in,
            "ssm_lam_re": ssm_lam_re, "ssm_lam_im": ssm_lam_im, "ssm_log_step": ssm_log_step,
            "ssm_b_re": ssm_b_re, "ssm_b_im": ssm_b_im, "ssm_c_re": ssm_c_re, "ssm_c_im": ssm_c_im,
            "ssm_d": ssm_d, "ssm_w_glu": ssm_w_glu, "ssm_b_glu": ssm_b_glu,
            "att_sinks": att_sinks,
            "gla_w_gate": gla_w_gate, "gla_b_gate": gla_b_gate, "gla_norm_g": gla_norm_g,
            "w_branch_ssm": w_branch_ssm, "w_branch_att": w_branch_att, "w_branch_gla": w_branch_gla,
            "w_out": w_out, "norm2_g": norm2_g, "w_ff1": w_ff1, "w_ff2": w_ff2,
            "final_norm_g": final_norm_g}


def reference(x, norm1_g, w_in, ssm_lam_re, ssm_lam_im, ssm_log_step, ssm_b_re, ssm_b_im,
              ssm_c_re, ssm_c_im, ssm_d, ssm_w_glu, ssm_b_glu, att_sinks, gla_w_gate, gla_b_gate,
              gla_norm_g, w_branch_ssm, w_branch_att, w_branch_gla, w_out, norm2_g, w_ff1, w_ff2,
              final_norm_g):
    bsz, s_len, _ = x.shape
    pos = jnp.arange(s_len, dtype=jnp.int32)
    splits = _split_points()
    for l in range(DEPTH):
        h = rms_norm(x, norm1_g[l])
        proj = h @ w_in[l]
        (u_ssm, aq, ak, av, gq, gk, gv, g_lr, g_out, merge) = jnp.split(proj, splits, axis=-1)
        y_ssm = s5_branch(u_ssm, ssm_lam_re[l], ssm_lam_im[l], ssm_log_step[l], ssm_b_re[l],
                          ssm_b_im[l], ssm_c_re[l], ssm_c_im[l], ssm_d[l], ssm_w_glu[l], ssm_b_glu[l])
        y_att = swa_branch(aq.reshape(bsz, s_len, ATT_HEADS, HEAD_DIM),
                           ak.reshape(bsz, s_len, ATT_KV_HEADS, HEAD_DIM),
                           av.reshape(bsz, s_len, ATT_KV_HEADS, HEAD_DIM), att_sinks[l], pos)
        y_gla = gla_branch(gq, gk, gv, g_lr, g_out, gla_w_gate[l], gla_b_gate[l], gla_norm_g[l])
        gates = jax.nn.sigmoid(merge).reshape(bsz, s_len, 3, D_MODEL)
        mixed = (gates[:, :, 0] * (y_ssm @ w_branch_ssm[l])
                 + gates[:, :, 1] * (y_att @ w_branch_att[l])
                 + gates[:, :, 2] * (y_gla @ w_branch_gla[l]))
        x = x + mixed @ w_out[l]
        h2 = rms_norm(x, norm2_g[l])
        x = x + jnp.square(jax.nn.relu(h2 @ w_ff1[l])) @ w_ff2[l]
    return rms_norm(x, final_norm_g)
```

```python
import contextlib
import numpy as np
import ml_dtypes
import concourse.bass as bass
import concourse.mybir as mybir
from concourse.bass_utils import run_bass_kernel_spmd

F32 = mybir.dt.float32
BF16 = mybir.dt.bfloat16
AF = mybir.ActivationFunctionType
ALU = mybir.AluOpType

NCORES = 8
T = 2048
D = 2048
NTG = 4
EPS = 1e-6
ENGS = ("tensor", "vector", "scalar", "gpsimd", "sync")
NDMASEM = {"tensor": 1, "vector": 1, "scalar": 1, "gpsimd": 1, "sync": 8}


class Prog:
    def __init__(self, nc, same_engine_sync=True):
        self.nc = nc
        self.q = {e: [] for e in ENGS}
        self.cnt = {e: 0 for e in ENGS}
        self.seen = {e: {} for e in ENGS}
        self.lw = {}
        self.rd = {}
        self.sems = {}
        self.dma_val = {}
        self.dma_rr = {e: 0 for e in ENGS}
        self.same = same_engine_sync
        self.n_inst = 0

    def _need(self, eng, tok):
        if tok is None:
            return
        key, val = tok
        if key == eng and (eng == "tensor" or not self.same):
            return
        if self.seen[eng].get(key, 0) >= val:
            return
        self.seen[eng][key] = val
        self.q[eng].append(("wait", key, val))

    def _deps(self, eng, reads, writes):
        for k in reads:
            self._need(eng, self.lw.get(k))
        for k in writes:
            self._need(eng, self.lw.get(k))
            for key, val in self.rd.get(k, {}).items():
                self._need(eng, (key, val))

    def _commit(self, tok, reads, writes):
        for k in writes:
            self.lw[k] = tok
            self.rd[k] = {}
        for k in reads:
            d = self.rd.setdefault(k, {})
            if d.get(tok[0], 0) < tok[1]:
                d[tok[0]] = tok[1]

    def op(self, eng, fn, reads=(), writes=()):
        self._deps(eng, reads, writes)
        self.cnt[eng] += 1
        tok = (eng, self.cnt[eng])
        self.q[eng].append(("op", fn))
        self._commit(tok, reads, writes)
        self.n_inst += 1
        return tok

    def dma(self, eng, out, in_, reads=(), writes=(), **kw):
        self._deps(eng, reads, writes)
        i = self.dma_rr[eng]
        self.dma_rr[eng] = (i + 1) % NDMASEM[eng]
        key = ("dma", eng, i)
        prev = self.dma_val.get(key, 0)
        if prev:
            self._need(eng, (key, prev))
        val = prev + 16
        self.dma_val[key] = val
        self.q[eng].append(("dma", key, out, in_, kw))
        tok = (key, val)
        self._commit(tok, reads, writes)
        self.n_inst += 1
        return tok

    def barrier(self):
        toks = [(e, self.cnt[e]) for e in ENGS if self.cnt[e]]
        toks += [(k, v) for k, v in self.dma_val.items()]
        for e in ENGS:
            for t in toks:
                if t[0] == e:
                    if e != "tensor" and self.same:
                        self._need(e, t)
                    continue
                self._need(e, t)
        self.lw = {}
        self.rd = {}

    def emit(self):
        nc = self.nc
        with contextlib.ExitStack() as st:
            for e in ENGS:
                self.sems[e] = st.enter_context(nc.semaphore("s_" + e))
            for k in list(self.dma_val.keys()):
                self.sems[k] = st.enter_context(nc.semaphore("d_%s_%s" % (k[1], k[2])))
            block = st.enter_context(nc.Block())

            def mk(ename):
                def body(eng):
                    own = self.sems[ename]
                    for item in self.q[ename]:
                        if item[0] == "wait":
                            eng.wait_ge(self.sems[item[1]], item[2])
                        elif item[0] == "op":
                            item[1](eng).then_inc(own, 1)
                        else:
                            _, key, out, in_, kw = item
                            eng.dma_start(out=out, in_=in_, **kw).then_inc(self.sems[key], 16)
                return body

            for e in ENGS:
                if self.q[e]:
                    getattr(block, e)(mk(e))


C_IDF = 0
C_PROT = 128
C_BAND = 256
C_TRI = 512
C_CHM = 576
C_SMASK = 1088
C_HALF = 1096
NCONST = 1104


def host_consts():
    c = np.zeros((128, NCONST), np.float32)
    c[:, C_IDF:C_IDF + 128] = np.eye(128, dtype=np.float32)
    pr = np.zeros((128, 128), np.float32)
    for b in (0, 64):
        for i in range(32):
            pr[b + i + 32, b + i] = -1.0
            pr[b + i, b + i + 32] = 1.0
    c[:, C_PROT:C_PROT + 128] = pr
    kj = np.arange(128)[:, None]
    qi = np.arange(128)[None, :]
    c[:, C_BAND:C_BAND + 128] = (kj > qi)
    c[:, C_BAND + 128:C_BAND + 256] = (kj <= qi)
    c[:, C_TRI:C_TRI + 64] = ((np.arange(128)[:, None] % 64) <= np.arange(64)[None, :])
    c[:, C_CHM:C_CHM + 512] = (np.arange(512)[None, :] % 64 != 0)
    c[:, C_SMASK:C_SMASK + 8] = (np.arange(128)[:, None] // 16 == np.arange(8)[None, :])
    c[:, C_HALF] = (np.arange(128) < 64)
    c[:, C_HALF + 1] = (np.arange(128) >= 64)
    return c


def rope_tables(core):
    half = 32
    inv = (np.float32(10000.0) ** (-np.arange(half, dtype=np.float32) / np.float32(half))).astype(np.float32)
    pos = (np.arange(T, dtype=np.float32) + np.float32(core * T)).astype(np.float32)
    ang = (pos[None, :] * inv[:, None]).astype(np.float32)
    cs = np.cos(ang.astype(np.float64)).astype(np.float32)
    sn = np.sin(ang.astype(np.float64)).astype(np.float32)
    tab = np.zeros((128, 2, T), np.float32)
    for p in range(128):
        tab[p, 0] = cs[p % 32]
        tab[p, 1] = sn[p % 32]
    return tab


class StopBuild(Exception):
    pass


class Builder:
    def trunc(self, n):
        import os
        if os.environ.get("TRUNC") == str(n):
            raise StopBuild()

    def __init__(self, mode, last=False, dbg=False):
        self.dbg_on = dbg
        self.mode = mode
        self.last = last
        self.nc = bass.Bass("TRN2", target_bir_lowering=False)
        import os
        if os.environ.get("PRECOOK", "0") == "0":
            try:
                self.nc.dge_precook = False
            except Exception as ex:
                print("cannot disable dge_precook:", ex)
        self.P = Prog(self.nc)
        self.in_names = []
        self.out_names = []
        self.ps = [self.nc.alloc_psum_tensor("ps%d" % i, [128, 512], F32) for i in range(8)]
        self.wrr = 0
        self.final_keys = []
        self.bf_inputs = set()
        self.d_x = None

    def din(self, name, shape):
        self.in_names.append(name)
        return self.nc.dram_tensor(name, list(shape), F32, kind="ExternalInput").ap()

    def dout(self, name, shape):
        self.out_names.append(name)
        return self.nc.dram_tensor(name, list(shape), F32, kind="ExternalOutput").ap()

    def dout_bf(self, name, shape):
        self.out_names.append(name)
        return self.nc.dram_tensor(name, list(shape), BF16, kind="ExternalOutput").ap()

    def din_bf(self, name, shape):
        self.in_names.append(name)
        self.bf_inputs.add(name)
        return self.nc.dram_tensor(name, list(shape), BF16, kind="ExternalInput").ap()

    def sb(self, st, name, shape, dt):
        self.wrr_n = getattr(self, "wrr_n", 0) + 1
        return st.enter_context(self.nc.sbuf_tensor("sb%d_%s" % (self.wrr_n, name), list(shape), dt))

    def dbg(self, name, ap, shape, dt=F32, reads=()):
        if not getattr(self, "dbg_on", False):
            return
        self.out_names.append(name)
        d = self.nc.dram_tensor(name, list(shape), dt, kind="ExternalOutput").ap()
        self.P.barrier()
        self.P.dma("sync", d, ap, writes=["dbg_" + name])
        self.P.barrier()

    def V(self, fn, r=(), w=()):
        return self.P.op("vector", fn, r, w)

    def A(self, fn, r=(), w=()):
        return self.P.op("scalar", fn, r, w)

    def G(self, fn, r=(), w=()):
        return self.P.op("gpsimd", fn, r, w)

    def M(self, fn, r=(), w=()):
        return self.P.op("tensor", fn, r, w)

    def load_w(self, wt, key, wd, c0, ncol, kc_n, r0=0, dst_c0=0):
        src = wd[r0:r0 + kc_n * 128, c0:c0 + ncol].rearrange("(k p) n -> p k n", p=128)
        self.P.dma("gpsimd", wt[:, 0:kc_n, dst_c0:dst_c0 + ncol], src, writes=[key])

    def mm(self, ps_ap, pskey, wt, wkey, col0, ncol, act_fn, kc_n):
        for k in range(kc_n):
            rhs, rkey = act_fn(k)
            self.M(lambda e, k=k, rhs=rhs: e.matmul(ps_ap, lhsT=wt[:, k, col0:col0 + ncol], rhs=rhs,
                                                    start=(k == 0), stop=(k == kc_n - 1)),
                   r=[wkey, rkey], w=[pskey])

    def hT_act(self, tg, n=512, t0=None):
        t0 = tg * 512 if t0 is None else t0
        return lambda k: (self.hT[:, k, t0:t0 + n], ("hT", k, tg))

    def setup_consts(self, st):
        P = self.P
        self.d_consts = self.din("consts", [128, NCONST])
        self.cst = self.sb(st, "cst", [128, NCONST], F32)
        P.dma("sync", self.cst[:], self.d_consts, writes=["cst"])
        self.ones_bf = self.sb(st, "ones_bf", [128, 128], BF16)
        self.V(lambda e: e.memset(self.ones_bf[:], 1.0), w=["ones"])
        self.ident_bf = self.sb(st, "ident_bf", [128, 128], BF16)
        self.V(lambda e: e.tensor_copy(out=self.ident_bf[:], in_=self.cst[:, C_IDF:C_IDF + 128]), r=["cst"], w=["identb"])
        self.eps_t = self.sb(st, "eps_t", [128, 1], F32)
        self.V(lambda e: e.memset(self.eps_t[:], EPS), w=["eps"])
        self.ident_f = self.cst[:, C_IDF:C_IDF + 128]

    def norm_tg(self, xsrc, xkey, g, gkey, out_fn, okey_fn, tgkey, st_tmp):
        sq, rs = st_tmp
        pss = self.ps[7][:]
        for kc in range(16):
            b = kc % 2
            x_ap = xsrc(kc)
            self.A(lambda e, x_ap=x_ap, b=b: e.activation(out=sq[:, b, :], in_=x_ap, func=AF.Square),
                   r=[xkey(kc)], w=[("sq", b)])
            self.M(lambda e, kc=kc, b=b: e.matmul(pss, lhsT=self.ones_bf[:], rhs=sq[:, b, :], start=(kc == 0), stop=(kc == 15)),
                   r=["ones", ("sq", b)], w=[("ps", 7)])
        self.A(lambda e: e.activation(out=rs[:], in_=pss, func=AF.Sqrt, bias=self.eps_t[:, 0:1], scale=1.0 / D),
               r=[("ps", 7), "eps"], w=["rs"])
        self.V(lambda e: e.reciprocal(out=rs[:], in_=rs[:]), r=["rs"], w=["rs"])
        for kc in range(16):
            x_ap = xsrc(kc)
            o_ap = out_fn(kc)
            self.V(lambda e, kc=kc, x_ap=x_ap, o_ap=o_ap: e.scalar_tensor_tensor(out=o_ap, in0=x_ap, scalar=g[:, kc:kc + 1], in1=rs[:],
                                                                                 op0=ALU.mult, op1=ALU.mult),
                   r=[xkey(kc), "rs", gkey], w=[okey_fn(kc)])

    def load_vec(self, st, name, dram, ncol):
        t = self.sb(st, name, [128, ncol], F32)
        self.P.dma("sync", t[:], dram.rearrange("(c p) -> p c", p=128), writes=[name], allow_slow_non_contiguous=True)
        return t

    def stage_norm1(self, st):
        P = self.P
        if self.d_x is None:
            self.d_x = self.din("xT", [D, T])
            self.d_n1g = self.din("norm1_g", [D])
        d_g = self.d_n1g
        self.hT = self.sb(st, "hT", [128, 16, T], BF16)
        with contextlib.ExitStack() as tmp:
            g = self.load_vec(tmp, "n1g", d_g, 16)
            xs = [self.sb(tmp, "xs%d" % i, [128, 16, 512], F32) for i in range(2)]
            sq = self.sb(tmp, "sq", [128, 2, 512], BF16)
            rs = self.sb(tmp, "rs", [128, 512], F32)
            xv = self.d_x.rearrange("(k p) t -> p k t", p=128)
            for tg in range(NTG):
                b = tg % 2
                P.dma("sync", xs[b][:], xv[:, :, tg * 512:(tg + 1) * 512], writes=[("xs", b)])
                self.norm_tg(lambda kc, b=b: xs[b][:, kc, :], lambda kc, b=b: ("xs", b), g, "n1g",
                             lambda kc, tg=tg: self.hT[:, kc, tg * 512:(tg + 1) * 512],
                             lambda kc, tg=tg: ("hT", kc, tg), tg, (sq, rs))
            P.barrier()

    @staticmethod
    def _k(x):
        if not isinstance(x, tuple):
            return []
        return list(x[1]) if isinstance(x[1], list) else [x[1]]

    @staticmethod
    def _a(x):
        return x[0] if isinstance(x, tuple) else x

    def tt(self, o, a, b, op, eng="vector"):
        self.P.op(eng, lambda e: e.tensor_tensor(out=o[0], in0=a[0], in1=b[0], op=op),
                  reads=self._k(a) + self._k(b), writes=self._k(o))

    def ts(self, o, a, s1, op0, s2=None, op1=None, eng="vector"):
        kw = dict(out=o[0], in0=a[0], scalar1=self._a(s1), scalar2=self._a(s2) if s2 is not None else None, op0=op0)
        if op1 is not None:
            kw["op1"] = op1
        self.P.op(eng, lambda e: e.tensor_scalar(**kw), reads=self._k(a) + self._k(s1) + self._k(s2), writes=self._k(o))

    def stt(self, o, a, s, b, op0, op1):
        self.P.op("vector", lambda e: e.scalar_tensor_tensor(out=o[0], in0=a[0], scalar=self._a(s), in1=b[0], op0=op0, op1=op1),
                  reads=self._k(a) + self._k(b) + self._k(s), writes=self._k(o))

    def act(self, o, a, func, bias=None, scale=None, accum=None):
        kw = dict(out=o[0], in_=a[0], func=func)
        rd = self._k(a)
        wr = self._k(o)
        if bias is not None:
            kw["bias"] = self._a(bias)
            rd += self._k(bias)
        if scale is not None:
            kw["scale"] = self._a(scale)
            rd += self._k(scale)
        if accum is not None:
            kw["accum_out"] = accum[0]
            wr += self._k(accum)
        self.P.op("scalar", lambda e: e.activation(**kw), reads=rd, writes=wr)

    def cp(self, o, a, eng="vector"):
        if eng == "scalar":
            self.P.op(eng, lambda e: e.copy(out=o[0], in_=a[0]), reads=self._k(a), writes=self._k(o))
        else:
            self.P.op(eng, lambda e: e.tensor_copy(out=o[0], in_=a[0]), reads=self._k(a), writes=self._k(o))

    def proj(self, wd, c0, ncols, kc_n, act_fn_tg, consumer, psbanks, tgs=range(NTG), ntok=512, dup=False, wcols=None):
        wt = self.wts[self.wrr % len(self.wts)]
        wkey = ("wt", self.wrr % len(self.wts))
        self.wrr += 1
        if dup:
            self.load_w(wt, wkey, wd, c0, 64, kc_n, dst_c0=0)
            self.load_w(wt, wkey, wd, c0, 64, kc_n, dst_c0=64)
            ncols = 128
        else:
            self.load_w(wt, wkey, wd, c0, ncols, kc_n)
        nb = 0
        for cb in range((ncols + 127) // 128):
            ncol = min(128, ncols - cb * 128)
            for tg in tgs:
                bank = psbanks[nb % len(psbanks)]
                nb += 1
                ps_ap = self.ps[bank][0:ncol, 0:ntok]
                self.mm(ps_ap, ("ps", bank), wt, wkey, cb * 128, ncol, act_fn_tg(tg), kc_n)
                consumer(cb, tg, ps_ap, ("ps", bank))

    def stage_s5(self, uT):
        P = self.P
        B = self.mode == "B"
        d_lr = self.din("ssm_lam_re", [64, 64])
        d_li = self.din("ssm_lam_im", [64, 64])
        d_ls = self.din("ssm_log_step", [64])
        d_bre = self.din("ssm_b_re", [64, 64, 16])
        d_bim = self.din("ssm_b_im", [64, 64, 16])
        d_cre = self.din("ssm_c_re", [64, 16, 64])
        d_cim = self.din("ssm_c_im", [64, 16, 64])
        if B:
            d_dsk = self.din("ssm_d", [1024])
            d_wglu = self.din("ssm_w_glu", [1024, 1024])
            d_bglu = self.din("ssm_b_glu", [1024])
            d_fins = self.din("s5_fins", [NCORES, 128, 64])
            self.yS = self.dout_bf("yS", [1024, T])
        else:
            d_fin = self.dout("s5_fin", [128, 64])
        MUL, ADD, SUB = ALU.mult, ALU.add, ALU.subtract
        with contextlib.ExitStack() as st:
            sb = lambda n, shp, dt=F32: self.sb(st, n, shp, dt)
            Xl_re = sb("Xl_re", [128, 32, 128], BF16)
            Xl_im = sb("Xl_im", [128, 32, 128], BF16)
            ekc = sb("ekc", [128, 32, 12])
            eks = sb("eks", [128, 32, 12])
            rpw = sb("rpw", [128, 32, 12])
            fin = sb("fin", [128, 64])
            if B:
                Ym_re = sb("Ym_re", [128, 32, 128], BF16)
                Ym_im = sb("Ym_im", [128, 32, 128], BF16)
                winit = sb("winit", [128, 64])
                dsk = self.load_vec(st, "dsk", d_dsk, 8)
                bglu = self.load_vec(st, "bglu", d_bglu, 8)
            with contextlib.ExitStack() as tmp:
                n_ = [0]

                def f(shape=(128, 32, 1)):
                    n_[0] += 1
                    nm = "s5t%d" % n_[0]
                    return (self.sb(tmp, nm, list(shape), F32)[:], nm)
                lr, li, ls = f(), f(), f()
                v3 = lambda ap: ap
                P.dma("sync", lr[0], d_lr.rearrange("(j s) (p o) -> (s p) j o", s=2, o=1), writes=[lr[1]], allow_slow_non_contiguous=True)
                P.dma("sync", li[0], d_li.rearrange("(j s) (p o) -> (s p) j o", s=2, o=1), writes=[li[1]], allow_slow_non_contiguous=True)
                lsv = d_ls.rearrange("(j s o) -> s j o", s=2, o=1)
                for s in range(2):
                    P.dma("sync", ls[0][64 * s:64 * s + 64], lsv[s:s + 1].to_broadcast([64, 32, 1]), writes=[ls[1]],
                          allow_slow_non_contiguous=True)
                step, lrs, lis, mag, sn, cs, t1, t2, t3 = f(), f(), f(), f(), f(), f(), f(), f(), f()
                self.act(step, ls, AF.Exp)
                self.tt(lrs, lr, step, MUL)
                self.tt(lis, li, step, MUL)
                self.ts(mag, lrs, 1.0 / 720, MUL)
                for cc in (1.0 / 120, 1.0 / 24, 1.0 / 6, 0.5, 1.0):
                    self.stt(mag, mag, cc, lrs, ADD, MUL)
                self.ts(mag, mag, 1.0, ADD)
                ph, zz, pp_, dd, ss_ = f(), f(), f(), f(), f()
                self.ts(ph, lis, 1.0 / 64, MUL)
                self.tt(zz, ph, ph, MUL)
                self.ts(pp_, zz, 1.0 / 362880, MUL)
                for cc in (-1.0 / 5040, 1.0 / 120, -1.0 / 6):
                    self.stt(pp_, pp_, cc, zz, ADD, MUL)
                self.stt(ss_, pp_, 1.0, ph, ADD, MUL)
                self.ts(pp_, zz, 1.0 / 40320, MUL)
                for cc in (-1.0 / 720, 1.0 / 24):
                    self.stt(pp_, pp_, cc, zz, ADD, MUL)
                self.stt(dd, pp_, -0.5, zz, ADD, MUL)

                def dsq():
                    self.stt(t1, dd, 2.0, dd, ADD, MUL)
                    self.tt(t2, ss_, ss_, MUL)
                    self.stt(t3, dd, 1.0, ss_, ADD, MUL)
                    self.tt(dd, t1, t2, SUB)
                    self.ts(ss_, t3, 2.0, MUL)
                for _ in range(6):
                    dsq()
                e0c = (ekc[:, :, 0:1], "ekc")
                e0s = (eks[:, :, 0:1], "eks")
                for k in range(12):
                    if k:
                        dsq()
                    self.ts((ekc[:, :, k:k + 1], "ekc"), dd, 1.0, ADD)
                    self.cp((eks[:, :, k:k + 1], "eks"), ss_)
                self.cp((rpw[:, :, 0:1], "rpw"), mag)
                for k in range(1, 12):
                    self.tt((rpw[:, :, k:k + 1], "rpw"), (rpw[:, :, k - 1:k], "rpw"), (rpw[:, :, k - 1:k], "rpw"), MUL)
                ar, ai, am1, den, fr, fi = f(), f(), f(), f(), f(), f()
                self.tt(ar, mag, e0c, MUL)
                self.tt(ai, mag, e0s, MUL)
                self.ts(am1, ar, -1.0, ADD)
                self.tt(t1, lr, lr, MUL)
                self.tt(t2, li, li, MUL)
                self.tt(den, t1, t2, ADD)
                self.P.op("vector", lambda e: e.reciprocal(out=den[0], in_=den[0]), reads=[den[1]], writes=[den[1]])
                self.tt(t1, am1, lr, MUL)
                self.tt(t2, ai, li, MUL)
                self.tt(t1, t1, t2, ADD)
                self.tt(fr, t1, den, MUL)
                self.tt(t1, ai, lr, MUL)
                self.tt(t2, am1, li, MUL)
                self.tt(t1, t1, t2, SUB)
                self.tt(fi, t1, den, MUL)
                Bre, Bim, Bbr, Bbi, tb1, tb2 = [f((128, 32, 16)) for _ in range(6)]
                P.dma("sync", Bre[0], d_bre.rearrange("(j s) p c -> (s p) j c", s=2), writes=[Bre[1]])
                P.dma("sync", Bim[0], d_bim.rearrange("(j s) p c -> (s p) j c", s=2), writes=[Bim[1]])
                bc = lambda x: (x[0].to_broadcast([128, 32, 16]), x[1])
                self.tt(tb1, Bre, bc(fr), MUL)
                self.tt(tb2, Bim, bc(fi), MUL)
                self.tt(Bbr, tb1, tb2, SUB)
                self.tt(tb1, Bim, bc(fr), MUL)
                self.tt(tb2, Bre, bc(fi), MUL)
                self.tt(Bbi, tb1, tb2, ADD)
                for (Bb, Xl, nm) in ((Bbr, Xl_re, "Xl_re"), (Bbi, Xl_im, "Xl_im")):
                    XLs = self.sb(tmp, "XLs_" + nm, [128, 32, 128], F32)
                    self.G(lambda e, XLs=XLs: e.memset(XLs[:], 0.0), w=["XLs" + nm])
                    dv = XLs[:].rearrange("p (o jj) c -> p o jj c", jj=4)
                    sv = Bb[0].rearrange("p (o jj) c -> p o jj c", jj=4)
                    for jj in range(4):
                        for s in range(2):
                            c0 = 32 * jj + 16 * s
                            self.cp((dv[64 * s:64 * s + 64, :, jj, c0:c0 + 16], "XLs" + nm),
                                    (sv[64 * s:64 * s + 64, :, jj, :], Bb[1]))
                    for g4 in range(8):
                        bank = g4 % 2
                        for i in range(4):
                            j = 4 * g4 + i
                            self.M(lambda e, j=j, i=i, bank=bank, XLs=XLs: e.transpose(self.ps[bank][:, i * 128:(i + 1) * 128], XLs[:, j, :], self.ident_f),
                                   r=["XLs" + nm, "cst"], w=[("ps", bank)])
                        self.cp((Xl[:, 4 * g4:4 * g4 + 4, :], nm), (self.ps[bank][:].rearrange("p (a b) -> p a b", b=128), ("ps", bank)), eng="scalar")
                if B:
                    for (dC, Ym, nm, sc) in ((d_cre, Ym_re, "Ym_re", 1.0), (d_cim, Ym_im, "Ym_im", -1.0)):
                        for o in range(8):
                            nat = self.sb(tmp, "nat%s%d" % (nm, o % 2), [128, 64], F32) if o < 2 else nat_l[o % 2]
                            if o < 2:
                                if o == 0:
                                    nat_l = [None, None]
                                    Z_l = [None, None]
                                nat_l[o] = nat
                                Z_l[o] = self.sb(tmp, "Z%s%d" % (nm, o), [128, 4, 128], F32)
                            Z = Z_l[o % 2]
                            nk = ("nat", nm, o % 2)
                            zk = ("Z", nm, o % 2)
                            P.dma("sync", nat[:], dC[8 * o:8 * o + 8].rearrange("g c p -> (g c) p"), writes=[nk])
                            for jj in range(4):
                                for s in range(2):
                                    col = C_SMASK + 2 * jj + s
                                    self.ts((Z[:, jj, 64 * s:64 * s + 64], zk), (nat[:], nk), (self.cst[:, col:col + 1], "cst"), MUL,
                                            eng="gpsimd" if s else "vector")
                            bank = 2 + (o % 2)
                            for jj in range(4):
                                self.M(lambda e, jj=jj, bank=bank, Z=Z: e.transpose(self.ps[bank][:, jj * 128:(jj + 1) * 128], Z[:, jj, :], self.ident_f),
                                       r=[zk, "cst"], w=[("ps", bank)])
                            self.act((Ym[:, 4 * o:4 * o + 4, :], nm), (self.ps[bank][:].rearrange("p (a b) -> p a b", b=128), ("ps", bank)),
                                     AF.Copy, scale=sc)
                    fins = self.sb(tmp, "fins", [128, NCORES, 64], F32)
                    P.dma("sync", fins[:], d_fins.rearrange("c p n -> p c n"), writes=["fins"])
                    cin = self.sb(tmp, "cin", [128, 64], F32)
                    self.V(lambda e: e.memset(cin[:], 0.0), w=["cin"])
                    Ar, Ai, u1, u2, nr, ni = [f((128, 32)) for _ in range(6)]
                    ek11c = (ekc[:, :, 11], "ekc")
                    ek11s = (eks[:, :, 11], "eks")
                    r11 = (rpw[:, :, 11], "rpw")
                    self.tt(Ar, r11, ek11c, MUL)
                    self.tt(Ai, r11, ek11s, MUL)
                    cr = (cin[:, 0:32], "cin")
                    ci = (cin[:, 32:64], "cin")
                    for c in range(NCORES - 1):
                        fr_ = (fins[:, c, 0:32], "fins")
                        fi_ = (fins[:, c, 32:64], "fins")
                        m = (self.cmask[:, c:c + 1], "cmask")
                        self.tt(u1, Ar, cr, MUL)
                        self.tt(u2, Ai, ci, MUL)
                        self.tt(nr, u1, u2, SUB)
                        self.tt(nr, nr, fr_, ADD)
                        self.tt(u1, Ar, ci, MUL)
                        self.tt(u2, Ai, cr, MUL)
                        self.tt(ni, u1, u2, ADD)
                        self.tt(ni, ni, fi_, ADD)
                        self.tt(nr, nr, cr, SUB)
                        self.tt(ni, ni, ci, SUB)
                        self.stt(cr, nr, m, cr, MUL, ADD)
                        self.stt(ci, ni, m, ci, MUL, ADD)
                    e0c2 = (ekc[:, :, 0], "ekc")
                    e0s2 = (eks[:, :, 0], "eks")
                    self.tt(u1, e0c2, cr, MUL)
                    self.tt(u2, e0s2, ci, MUL)
                    self.tt((winit[:, 0:32], "winit"), u1, u2, SUB)
                    self.tt(u1, e0c2, ci, MUL)
                    self.tt(u2, e0s2, cr, MUL)
                    self.tt((winit[:, 32:64], "winit"), u1, u2, ADD)
                P.barrier()
            self.trunc(2)
            self.dbg("dbg_ekc", ekc[:], [128, 32, 12])
            self.dbg("dbg_eks", eks[:], [128, 32, 12])
            self.dbg("dbg_rpw", rpw[:], [128, 32, 12])
            self.dbg("dbg_Xl_re", Xl_re[:], [128, 32, 128], BF16)
            self.dbg("dbg_Xl_im", Xl_im[:], [128, 32, 128], BF16)
            if B:
                zT = sb("zT", [128, 8, T], BF16)
            with contextlib.ExitStack() as tmp:
                E = [self.sb(tmp, "E%d" % i, [128, 2, T], F32) for i in range(2)]
                tA = self.sb(tmp, "tA", [128, 1024], F32)
                w_in = self.sb(tmp, "w_in", [128, 2, 512], F32)
                tq = [self.sb(tmp, "tq%d" % i, [128, 512], F32) for i in range(4)]
                wsc = [self.sb(tmp, "wsc%d" % i, [128, 2, 512], F32) for i in range(2)]
                pp = [self.sb(tmp, "pp%d" % i, [128, 512], F32) for i in range(4)]
                srt = [self.sb(tmp, "srt%d" % i, [128, 2, 512], BF16) for i in range(2)]
                ytmp = self.sb(tmp, "ytmp", [128, 512], F32)
                for j in range(32):
                    o, jj = j // 4, j % 4
                    Eb = E[j % 2]
                    ek = ("E", j % 2)
                    Er, Ei = Eb[:, 0, :], Eb[:, 1, :]
                    self.G(lambda e, Er=Er: e.memset(Er[:, 0:1], 1.0), w=[ek])
                    self.G(lambda e, Ei=Ei: e.memset(Ei[:, 0:1], 0.0), w=[ek])
                    for k in range(11):
                        n = 1 << k
                        c_ = (ekc[:, j, k:k + 1], "ekc")
                        s_ = (eks[:, j, k:k + 1], "eks")
                        self.ts((tA[:, 0:n], "tA"), (Ei[:, 0:n], ek), s_, MUL, eng="gpsimd")
                        self.ts((Er[:, n:2 * n], ek), (Er[:, 0:n], ek), c_, MUL, eng="gpsimd")
                        self.tt((Er[:, n:2 * n], ek), (Er[:, n:2 * n], ek), (tA[:, 0:n], "tA"), SUB, eng="gpsimd")
                        self.ts((tA[:, 0:n], "tA"), (Er[:, 0:n], ek), s_, MUL, eng="gpsimd")
                        self.ts((Ei[:, n:2 * n], ek), (Ei[:, 0:n], ek), c_, MUL, eng="gpsimd")
                        self.tt((Ei[:, n:2 * n], ek), (Ei[:, n:2 * n], ek), (tA[:, 0:n], "tA"), ADD, eng="gpsimd")
                    if j == 0:
                        self.dbg("dbg_E0", Eb[:], [128, 2, T])
                    rbc = (rpw[:, j, 0:1].to_broadcast([128, 512]), "rpw")
                    for tg in range(NTG):
                        sl = slice(tg * 512, (tg + 1) * 512)
                        bx = 4 + 2 * (tg % 2)
                        Xr = (self.ps[bx][:], ("ps", bx))
                        Xi = (self.ps[bx + 1][:], ("ps", bx + 1))
                        self.M(lambda e, j=j, o=o, sl=sl, bx=bx: e.matmul(self.ps[bx][:], lhsT=Xl_re[:, j, :], rhs=uT[:, o, sl], start=True, stop=True),
                               r=["Xl_re", ("uT", o, tg)], w=[("ps", bx)])
                        self.M(lambda e, j=j, o=o, sl=sl, bx=bx: e.matmul(self.ps[bx + 1][:], lhsT=Xl_im[:, j, :], rhs=uT[:, o, sl], start=True, stop=True),
                               r=["Xl_im", ("uT", o, tg)], w=[("ps", bx + 1)])
                        c_ = (Er[:, sl], ek)
                        s_ = (Ei[:, sl], ek)
                        if j == 0 and tg == 3 and self.dbg_on:
                            self.cp((tq[0][:], ("tq", 0)), Xr)
                            self.cp((tq[1][:], ("tq", 1)), Xi)
                            self.dbg("dbg_Xr", tq[0][:], [128, 512])
                            self.dbg("dbg_Xi", tq[1][:], [128, 512])
                        q0, q1, q2, q3 = [(tq[i][:], ("tq", i)) for i in range(4)]
                        wri = (w_in[:, 0, :], ("w_in", 0))
                        wii = (w_in[:, 1, :], ("w_in", 1))
                        self.tt(q0, Xr, c_, MUL)
                        self.tt(q1, Xi, s_, MUL)
                        self.tt(wri, q0, q1, ADD)
                        self.tt(q2, Xi, c_, MUL)
                        self.tt(q3, Xr, s_, MUL)
                        self.tt(wii, q2, q3, SUB)
                        wb = wsc[tg % 2]
                        wpb = wsc[(tg + 1) % 2]
                        for ri in range(2):
                            if tg == 0:
                                init = winit[:, 32 * ri + j:32 * ri + j + 1] if B else 0.0
                                ird = ["winit"] if B else []
                            else:
                                init = wpb[:, ri, 511:512]
                                ird = [("wsc", (tg + 1) % 2, ri)]
                            self.V(lambda e, wb=wb, ri=ri, init=init, rbc=rbc: e.tensor_tensor_scan(
                                out=wb[:, ri, :], data0=rbc[0], data1=w_in[:, ri, :], initial=init, op0=MUL, op1=ADD),
                                r=["rpw", ("w_in", ri)] + ird, w=[("wsc", tg % 2, ri)])
                        if B:
                            wr = (wb[:, 0, :], ("wsc", tg % 2, 0))
                            wi = (wb[:, 1, :], ("wsc", tg % 2, 1))
                            p0, p1, p2, p3 = [(pp[i][:], ("pp", i)) for i in range(4)]
                            sb_ = srt[tg % 2]
                            sr = (sb_[:, 0, :], ("srt", tg % 2, 0))
                            si = (sb_[:, 1, :], ("srt", tg % 2, 1))
                            self.tt(p0, wr, c_, MUL, eng="gpsimd")
                            self.tt(p1, wi, s_, MUL, eng="gpsimd")
                            self.tt(sr, p0, p1, SUB)
                            self.tt(p2, wr, s_, MUL, eng="gpsimd")
                            self.tt(p3, wi, c_, MUL, eng="gpsimd")
                            self.tt(si, p2, p3, ADD)
                            self.M(lambda e, j=j, tg=tg, sb_=sb_, jj=jj: e.matmul(self.ps[tg][:], lhsT=Ym_re[:, j, :], rhs=sb_[:, 0, :], start=(jj == 0), stop=False),
                                   r=["Ym_re", sr[1]], w=[("ps", tg)])
                            self.M(lambda e, j=j, tg=tg, sb_=sb_, jj=jj: e.matmul(self.ps[tg][:], lhsT=Ym_im[:, j, :], rhs=sb_[:, 1, :], start=False, stop=(jj == 3)),
                                   r=["Ym_im", si[1]], w=[("ps", tg)])
                    if j == 0:
                        self.dbg("dbg_w3", wsc[(NTG - 1) % 2][:], [128, 2, 512])
                        self.dbg("dbg_win", w_in[:], [128, 2, 512])
                    if not B:
                        wl = wsc[(NTG - 1) % 2]
                        wre = (wl[:, 0, 511:512], ("wsc", (NTG - 1) % 2, 0))
                        wie = (wl[:, 1, 511:512], ("wsc", (NTG - 1) % 2, 1))
                        ce = (Er[:, T - 1:T], ek)
                        se = (Ei[:, T - 1:T], ek)
                        a0, a1 = (tq[0][:, 0:1], ("tq", 0)), (tq[1][:, 0:1], ("tq", 1))
                        self.tt(a0, ce, wre, MUL)
                        self.tt(a1, se, wie, MUL)
                        self.tt((fin[:, j:j + 1], "fin"), a0, a1, SUB)
                        self.tt(a0, se, wre, MUL)
                        self.tt(a1, ce, wie, MUL)
                        self.tt((fin[:, 32 + j:33 + j], "fin"), a0, a1, ADD)
                    elif jj == 3:
                        for tg in range(NTG):
                            sl = slice(tg * 512, (tg + 1) * 512)
                            self.stt((ytmp[:], "ytmp"), (uT[:, o, sl], ("uT", o, tg)), (dsk[:, o:o + 1], "dsk"),
                                     (self.ps[tg][:], ("ps", tg)), MUL, ADD)
                            self.act((zT[:, o, sl], ("zT", o, tg)), (ytmp[:], "ytmp"), AF.Gelu)
                P.barrier()
            self.trunc(3)
            if not B:
                P.dma("sync", d_fin, fin[:], reads=["fin"], writes=["d_fin"])
                self.final_keys.append("d_fin")
                P.barrier()
                return
            with contextlib.ExitStack() as tmp:
                self.wts = [self.sb(tmp, "wtG%d" % i, [128, 16, 512], BF16) for i in range(2)]
                sg = [self.sb(tmp, "sg%d" % i, [128, 512], F32) for i in range(2)]
                yo = [self.sb(tmp, "yo%d" % i, [128, 512], BF16) for i in range(2)]
                cnt = [0]
                for half in range(2):
                    def cons(cb, tg, ps_ap, pk, half=half):
                        c = half * 4 + cb
                        i = cnt[0] % 2
                        cnt[0] += 1
                        sl = slice(tg * 512, (tg + 1) * 512)
                        self.act((sg[i][:], ("sg", i)), (ps_ap, pk), AF.Sigmoid, bias=(bglu[:, c:c + 1], "bglu"))
                        self.tt((yo[i][:], ("yo", i)), (zT[:, c, sl], ("zT", c, tg)), (sg[i][:], ("sg", i)), MUL)
                        P.dma("sync", self.yS[c * 128:(c + 1) * 128, sl], yo[i][:], reads=[("yo", i)], writes=[("yS", c, tg)])
                    self.proj(d_wglu, half * 512, 512, 8, lambda tg: (lambda k: (zT[:, k, tg * 512:(tg + 1) * 512], ("zT", k, tg))),
                              cons, [4, 5, 6, 7])
                P.barrier()

    def stage_gla(self):
        P = self.P
        B = self.mode == "B"
        MUL, ADD, SUB = ALU.mult, ALU.add, ALU.subtract
        d_wgk = self.din("w_gk", [D, 512])
        d_wgv = self.din("w_gv", [D, 1024])
        d_wglr = self.din("w_glr", [D, 16])
        d_wgate = self.din("gla_w_gate", [16, 512])
        d_bgate = self.din("gla_b_gate", [512])
        if B:
            d_wgq = self.din("w_gq", [D, 512])
            d_wgo = self.din("w_gout", [D, 1024])
            d_ng = self.din("gla_norm_g", [1024])
            d_fins = self.din("gla_fins", [NCORES, 128, 1024])
            d_dts = self.din("gla_dtots", [NCORES, 128, 4])
            self.yG = self.dout_bf("yG", [1024, T])
        else:
            d_fin = self.dout("gla_fin", [128, 1024])
            d_dt = self.dout("gla_dtot", [128, 4])
        with contextlib.ExitStack() as st:
            sb = lambda n, shp, dt=F32: self.sb(st, n, shp, dt)
            self.wts = [sb("wtL%d" % i, [128, 16, 256], BF16) for i in range(2)]
            glr = sb("glr", [16, T], BF16)
            wg = sb("wg", [16, 512], BF16)
            negb = self.load_vec(st, "negb", d_bgate, 4)
            self.ts((negb[:], "negb"), (negb[:], "negb"), -1.0, MUL)
            P.dma("gpsimd", wg[:], d_wgate, writes=["wg"])
            S = sb("S", [128, 4, 256])
            dtot = sb("dtot", [128, 4])
            self.V(lambda e: e.memset(S[:], 0.0), w=[("S", h) for h in range(4)])
            self.V(lambda e: e.memset(dtot[:], 1.0), w=["dtot"])
            if B:
                Sbf = sb("Sbf", [128, 4, 256], BF16)
                ng = self.load_vec(st, "ng", d_ng, 8)
                dts = sb("dts", [128, NCORES, 4])
                P.dma("sync", dts[:], d_dts.rearrange("c p n -> p c n"), writes=["dts"])
                fb = [sb("gfin%d" % i, [128, 1024]) for i in range(2)]
                tmpc = sb("tmpc", [128, 256])
                for c in range(NCORES - 1):
                    fk = ("gfin", c % 2)
                    P.dma("sync", fb[c % 2][:], d_fins[c], writes=[fk])
                    for hd in range(4):
                        Sh = (S[:, hd, :], ("S", hd))
                        self.stt((tmpc[:], "tmpc"), Sh, (dts[:, c, hd:hd + 1], "dts"), (fb[c % 2][:, hd * 256:(hd + 1) * 256], fk), MUL, ADD)
                        self.tt((tmpc[:], "tmpc"), (tmpc[:], "tmpc"), Sh, SUB)
                        self.stt(Sh, (tmpc[:], "tmpc"), (self.cmask[:, c:c + 1], "cmask"), Sh, MUL, ADD)
                self.cp((Sbf[:], [("Sbf", h) for h in range(4)]), (S[:], [("S", h) for h in range(4)]), eng="scalar")
            def cons_glr(cb, tg, ps_ap, pk):
                self.cp((glr[:, tg * 512:(tg + 1) * 512], ("glr", tg)), (ps_ap, pk), eng="scalar")
            self.proj(d_wglr, 0, 16, 16, self.hT_act, cons_glr, [0, 1])
            e1 = sb("e1", [128, T])
            e2 = sb("e2", [128, T]) if B else None
            e3 = sb("e3", [128, T])
            dec = sb("dec", [128, 32])
            lt = sb("lt", [128, 512])
            cs = sb("cs", [128, 512])
            et = sb("et", [128, 512])
            kdT = sb("kdT", [128, T], BF16)
            kd_tm = sb("kd_tm", [128, 16, 128], BF16)
            v_tm = sb("v_tm", [128, 16, 256], BF16)
            vtmp = [sb("vtmp%d" % i, [128, 512], BF16) for i in range(2)]
            if B:
                qin = sb("qin", [128, T], BF16)
                kin = sb("kin", [128, T], BF16)
                sgT = sb("sgT", [128, 2, T], BF16)
                ygl = sb("ygl", [128, 2, T], BF16)
                attm = [sb("attm%d" % i, [128, 64], BF16) for i in range(2)]
                on_t = [sb("on_t%d" % i, [64, 256], BF16) for i in range(2)]
                junk = sb("junk", [64, 256])
                ss = sb("ss", [64, 2])
            psT = [self.ps[i][:].bitcast(BF16) for i in range(8)]
            for hd in range(4):
                for tg in range(NTG):
                    sl = slice(tg * 512, (tg + 1) * 512)
                    bank = 6 + tg % 2
                    self.M(lambda e, hd=hd, sl=sl, bank=bank: e.matmul(self.ps[bank][:], lhsT=wg[:, hd * 128:(hd + 1) * 128], rhs=glr[:, sl], start=True, stop=True),
                           r=["wg", ("glr", tg)], w=[("ps", bank)])
                    self.act((et[:], "et"), (self.ps[bank][:], ("ps", bank)), AF.Exp, bias=(negb[:, hd:hd + 1], "negb"), scale=-1.0)
                    self.act((lt[:], "lt"), (et[:], "et"), AF.Ln, bias=1.0)
                    self.V(lambda e: e.tensor_tensor_scan(out=cs[:], data0=self.cst[:, C_CHM:C_CHM + 512], data1=lt[:], initial=0.0, op0=MUL, op1=ADD),
                           r=["cst", "lt"], w=["cs"])
                    self.act((e1[:, sl], ("e1", tg)), (cs[:], "cs"), AF.Exp, scale=-1.0 / 16)
                    if B:
                        self.act((e2[:, sl], ("e2", tg)), (cs[:], "cs"), AF.Exp, scale=1.0 / 16)
                    cs3 = cs[:].rearrange("p (n t) -> p n t", t=64)
                    self.act((dec[:, 8 * tg:8 * tg + 8], "dec"), (cs3[:, :, 63], "cs"), AF.Exp, scale=-1.0 / 16)
                    self.tt((et[:].rearrange("p (n t) -> p n t", t=64), "et"), (cs3, "cs"), (cs3[:, :, 63:64].to_broadcast([128, 8, 64]), "cs"), SUB)
                    self.act((e3[:, sl], ("e3", tg)), (et[:], "et"), AF.Exp, scale=1.0 / 16)
                if B:
                    def cons_q(cb, tg, ps_ap, pk):
                        sl = slice(tg * 512, (tg + 1) * 512)
                        self.stt((qin[:, sl], ("qin", tg)), (ps_ap, pk), float(128 ** -0.5), (e1[:, sl], ("e1", tg)), MUL, MUL)
                    self.proj(d_wgq, hd * 128, 128, 16, self.hT_act, cons_q, [0, 1, 2, 3])

                def cons_k(cb, tg, ps_ap, pk):
                    sl = slice(tg * 512, (tg + 1) * 512)
                    if B:
                        self.tt((kin[:, sl], ("kin", tg)), (ps_ap, pk), (e2[:, sl], ("e2", tg)), MUL)
                    self.tt((kdT[:, sl], ("kdT", tg)), (ps_ap, pk), (e3[:, sl], ("e3", tg)), MUL)
                    bank = 4 + tg % 2
                    for i in range(4):
                        blk = tg * 4 + i
                        self.M(lambda e, blk=blk, i=i, bank=bank: e.transpose(psT[bank][:, i * 128:(i + 1) * 128], kdT[:, blk * 128:(blk + 1) * 128], self.ident_bf[:]),
                               r=[("kdT", tg), "identb"], w=[("ps", bank)])
                    self.cp((kd_tm[:, 4 * tg:4 * tg + 4, :], ("kd_tm", tg)), (psT[bank][:, 0:512].rearrange("p (a b) -> p a b", b=128), ("ps", bank)), eng="scalar")
                self.proj(d_wgk, hd * 128, 128, 16, self.hT_act, cons_k, [0, 1, 2, 3])
                vc = [0]

                def cons_v(cb, tg, ps_ap, pk):
                    i2 = vc[0] % 2
                    vc[0] += 1
                    self.cp((vtmp[i2][:], ("vtmp", i2)), (ps_ap, pk), eng="scalar")
                    bank = 4 + i2
                    for i in range(4):
                        self.M(lambda e, i=i, i2=i2, bank=bank: e.transpose(psT[bank][:, i * 128:(i + 1) * 128], vtmp[i2][:, i * 128:(i + 1) * 128], self.ident_bf[:]),
                               r=[("vtmp", i2), "identb"], w=[("ps", bank)])
                    self.cp((v_tm[:, 4 * tg:4 * tg + 4, cb * 128:(cb + 1) * 128], ("v_tm", tg)),
                            (psT[bank][:, 0:512].rearrange("p (a b) -> p a b", b=128), ("ps", bank)))
                self.proj(d_wgv, hd * 256, 256, 16, self.hT_act, cons_v, [0, 1, 2, 3])
                if B:
                    def cons_go(cb, tg, ps_ap, pk):
                        self.act((sgT[:, cb, tg * 512:(tg + 1) * 512], ("sgT", tg)), (ps_ap, pk), AF.Silu)
                    self.proj(d_wgo, hd * 256, 256, 16, self.hT_act, cons_go, [0, 1, 2, 3])
                Sh = (S[:, hd, :], ("S", hd))
                for n in range(32):
                    tb, hf, tg = n // 2, n % 2, n // 8
                    rows = slice(64 * hf, 64 * hf + 64)
                    tok = slice(64 * n, 64 * n + 64)
                    par = n % 2
                    if B:
                        bA, bO, bT = par, 2 + par, 4 + par
                        self.M(lambda e, tb=tb, tok=tok, bA=bA: e.matmul(self.ps[bA][:, 0:64], lhsT=kin[:, tb * 128:(tb + 1) * 128], rhs=qin[:, tok], start=True, stop=True),
                               r=[("kin", tg), ("qin", tg)], w=[("ps", bA)])
                        self.tt((attm[par][rows, :], ("attm", par)), (self.ps[bA][rows, 0:64], ("ps", bA)), (self.cst[rows, C_TRI:C_TRI + 64], "cst"), MUL)
                        self.M(lambda e, rows=rows, tb=tb, par=par, bO=bO: e.matmul(self.ps[bO][0:64, 0:256], lhsT=attm[par][rows, :], rhs=v_tm[rows, tb, :], start=True, stop=False),
                               r=[("attm", par), ("v_tm", tg)], w=[("ps", bO)])
                        self.M(lambda e, tok=tok, hd=hd, bO=bO: e.matmul(self.ps[bO][0:64, 0:256], lhsT=qin[:, tok], rhs=Sbf[:, hd, :], start=False, stop=True),
                               r=[("qin", tg), ("Sbf", hd)], w=[("ps", bO)])
                        po = (self.ps[bO][0:64, 0:256], ("ps", bO))
                        self.act((junk[:], "junk"), po, AF.Square, accum=(ss[:, 0:1], "ss"))
                        self.act((ss[:, 1:2], "ss"), (ss[:, 0:1], "ss"), AF.Sqrt, bias=(self.eps_t[0:64, 0:1], "eps"), scale=1.0 / 256)
                        self.V(lambda e: e.reciprocal(out=ss[:, 1:2], in_=ss[:, 1:2]), r=["ss"], w=["ss"])
                        self.ts((on_t[par][:], ("on_t", par)), po, (ss[:, 1:2], "ss"), MUL)
                        for cb in range(2):
                            self.M(lambda e, cb=cb, par=par, bT=bT: e.transpose(psT[bT][:, cb * 64:cb * 64 + 64], on_t[par][:, cb * 128:(cb + 1) * 128], self.ident_bf[0:64, 0:64]),
                                   r=[("on_t", par), "identb"], w=[("ps", bT)])
                        for cb in range(2):
                            self.stt((ygl[:, cb, tok], ("ygl", tg)), (psT[bT][:, cb * 64:cb * 64 + 64], ("ps", bT)), (ng[:, hd * 2 + cb:hd * 2 + cb + 1], "ng"),
                                     (sgT[:, cb, tok], ("sgT", tg)), MUL, MUL)
                    bU = 6 + par
                    self.M(lambda e, rows=rows, tb=tb, bU=bU: e.matmul(self.ps[bU][:, 0:256], lhsT=kd_tm[rows, tb, :], rhs=v_tm[rows, tb, :], start=True, stop=True),
                           r=[("kd_tm", tg), ("v_tm", tg)], w=[("ps", bU)])
                    self.stt(Sh, Sh, (dec[:, n:n + 1], "dec"), (self.ps[bU][:, 0:256], ("ps", bU)), MUL, ADD)
                    if B:
                        self.cp((Sbf[:, hd, :], ("Sbf", hd)), Sh, eng="scalar")
                    else:
                        self.tt((dtot[:, hd:hd + 1], "dtot"), (dtot[:, hd:hd + 1], "dtot"), (dec[:, n:n + 1], "dec"), MUL)
                if B:
                    P.dma("sync", self.yG[hd * 256:(hd + 1) * 256, :].rearrange("(c p) t -> p c t", p=128), ygl[:],
                          reads=[("ygl", t_) for t_ in range(4)], writes=[("yG", hd)])
            if not B:
                P.dma("sync", d_fin, S[:].rearrange("p h e -> p (h e)"), reads=[("S", h) for h in range(4)], writes=["d_gfin"])
                P.dma("sync", d_dt, dtot[:], reads=["dtot"], writes=["d_gdt"])
                self.final_keys += ["d_gfin", "d_gdt"]
            P.barrier()

    def stage_swa(self):
        P = self.P
        B = self.mode == "B"
        MUL, ADD = ALU.mult, ALU.add
        d_wak = self.din("w_ak", [D, 256])
        d_wav = self.din("w_av", [D, 256])
        d_rope = self.din("rope", [128, 2, T])
        if B:
            d_waq = self.din("w_aq", [D, 1024])
            d_sinks = self.din("att_sinks", [16])
            d_hk = self.din("halo_k", [128, 4, 128])
            d_hv = self.din("halo_v", [128, 4, 128])
            d_hp = self.din("hasprev", [128, 1])
            self.yA = self.dout_bf("yA", [1024, T])
        else:
            d_hko = self.dout("halo_ko", [128, 4, 128])
            d_hvo = self.dout("halo_vo", [128, 4, 128])
        tgs = range(NTG) if B else [NTG - 1]
        with contextlib.ExitStack() as st:
            sb = lambda n, shp, dt=F32: self.sb(st, n, shp, dt)
            self.wts = [sb("wtS%d" % i, [128, 16, 512], BF16) for i in range(2)]
            rope = sb("rope_t", [128, 2, T])
            P.dma("sync", rope[:], d_rope, writes=["rope"])
            kT = sb("kT", [128, 4, T], BF16)
            v2 = sb("v2", [128, 16, 4, 128], BF16)
            xf = [sb("xf%d" % i, [128, 512]) for i in range(2)]
            r1 = [sb("r1%d" % i, [128, 512]) for i in range(2)]
            r2 = [sb("r2%d" % i, [128, 512]) for i in range(2)]
            vtmp = [sb("vtmpS%d" % i, [128, 512], BF16) for i in range(2)]
            prot = self.cst[:, C_PROT:C_PROT + 128]
            psT = [self.ps[i][:].bitcast(BF16) for i in range(8)]
            rc = [0]

            def rope_cons(dst_fn):
                def cons(cb, tg, ps_ap, pk):
                    i = rc[0] % 2
                    rc[0] += 1
                    sl = slice(tg * 512, (tg + 1) * 512)
                    bR = 4 + i
                    self.cp((xf[i][:], ("xf", i)), (ps_ap, pk), eng="scalar")
                    self.M(lambda e, i=i, bR=bR: e.matmul(self.ps[bR][:], lhsT=prot, rhs=xf[i][:], start=True, stop=True),
                           r=["cst", ("xf", i)], w=[("ps", bR)])
                    self.tt((r1[i][:], ("r1", i)), (xf[i][:], ("xf", i)), (rope[:, 0, sl], "rope"), MUL, eng="gpsimd")
                    self.tt((r2[i][:], ("r2", i)), (self.ps[bR][:], ("ps", bR)), (rope[:, 1, sl], "rope"), MUL)
                    d_ap, d_key = dst_fn(cb, tg)
                    self.tt((d_ap, d_key), (r1[i][:], ("r1", i)), (r2[i][:], ("r2", i)), ADD)
                return cons
            for g in range(4):
                self.proj(d_wak, g * 64, 64, 16, self.hT_act,
                          rope_cons(lambda cb, tg, g=g: (kT[:, g, tg * 512:(tg + 1) * 512], ("kT", g, tg))), [0, 1, 2, 3], tgs=tgs, dup=True)
            vc = [0]
            for g in range(4):
                def cons_v(cb, tg, ps_ap, pk, g=g):
                    i2 = vc[0] % 2
                    vc[0] += 1
                    self.cp((vtmp[i2][:], ("vtmpS", i2)), (ps_ap, pk), eng="scalar")
                    bank = 6 + i2
                    for i in range(4):
                        self.M(lambda e, i=i, i2=i2, bank=bank: e.transpose(psT[bank][:, i * 128:(i + 1) * 128], vtmp[i2][:, i * 128:(i + 1) * 128], self.ident_bf[:]),
                               r=[("vtmpS", i2), "identb"], w=[("ps", bank)])
                    self.cp((v2[:, 4 * tg:4 * tg + 4, g, :], ("v2", g, tg)),
                            (psT[bank][:, 0:512].rearrange("p (a b) -> p a b", b=128), ("ps", bank)))
                self.proj(d_wav, g * 64, 64, 16, self.hT_act, cons_v, [0, 1, 2, 3], tgs=tgs, dup=True)
            if not B:
                hko = sb("hko", [128, 4, 128])
                hvo = sb("hvo", [128, 4, 128])
                self.cp((hko[:], "hko"), (kT[:, :, T - 128:T], [("kT", g, NTG - 1) for g in range(4)]))
                self.cp((hvo[:], "hvo"), (v2[:, 15, :, :], [("v2", g, NTG - 1) for g in range(4)]))
                P.dma("sync", d_hko, hko[:], reads=["hko"], writes=["d_hko"])
                P.dma("sync", d_hvo, hvo[:], reads=["hvo"], writes=["d_hvo"])
                self.final_keys += ["d_hko", "d_hvo"]
                P.barrier()
                return
            qT = sb("qT", [128, 8, T], BF16)
            for half in range(2):
                self.proj(d_waq, half * 512, 512, 16, self.hT_act,
                          rope_cons(lambda cb, tg, half=half: (qT[:, half * 4 + cb, tg * 512:(tg + 1) * 512], [("qT", half * 4 + cb, 4 * tg + b_) for b_ in range(4)])),
                          [0, 1, 2, 3])
            self.trunc(51)
            hk = sb("hk", [128, 4, 128], BF16)
            hv = sb("hv", [128, 4, 128], BF16)
            P.dma("gpsimd", hk[:], d_hk, writes=["hk"])
            P.dma("gpsimd", hv[:], d_hv, writes=["hv"])
            hp = sb("hp", [128, 1])
            P.dma("sync", hp[:], d_hp, writes=["hp"])
            esk = sb("esk", [128, 16, 1])
            P.dma("sync", esk[:], d_sinks.rearrange("(o h u) -> o h u", o=1, u=1).to_broadcast([128, 16, 1]), writes=["esk"],
                  allow_slow_non_contiguous=True)
            self.act((esk[:], "esk"), (esk[:], "esk"), AF.Exp)
            band = sb("band", [128, 2, 1, 128], BF16)
            band0 = sb("band0", [128, 2, 1, 128], BF16)
            bsrc = self.cst[:, C_BAND:C_BAND + 256].rearrange("p (a o q) -> p a o q", a=2, o=1)
            self.cp((band[:], "band"), (bsrc, "cst"))
            self.ts((band0[:, 0], "band0"), (bsrc[:, 0], "cst"), (hp[:, 0:1], "hp"), MUL)
            self.cp((band0[:, 1], "band0"), (bsrc[:, 1], "cst"))
            self.trunc(52)
            pt = [sb("pt%d" % i, [128, 2, 4, 128], BF16) for i in range(2)]
            den = [sb("den%d" % i, [128, 4, 128]) for i in range(2)]
            it = 0
            for blk in range(16):
                qs = slice(blk * 128, (blk + 1) * 128)
                for g in range(4):
                    par = it % 2
                    it += 1
                    bO = 4 + 2 * par
                    bD = 5 + 2 * par
                    for kb in range(2):
                        for hh in range(4):
                            h = 4 * g + hh
                            cb, hf = h // 2, h % 2
                            rows = slice(64 * hf, 64 * hf + 64)
                            bank = 2 * kb + hf
                            col = (hh // 2) * 128
                            if kb == 1:
                                ksrc, kkey = kT[rows, g, qs], ("kT", g, blk // 4)
                            elif blk > 0:
                                ksrc, kkey = kT[rows, g, (blk - 1) * 128:blk * 128], ("kT", g, (blk - 1) // 4)
                            else:
                                ksrc, kkey = hk[rows, g, :], "hk"
                            self.M(lambda e, ksrc=ksrc, rows=rows, cb=cb, qs=qs, bank=bank, col=col: e.matmul(
                                self.ps[bank][:, col:col + 128], lhsT=ksrc, rhs=qT[rows, cb, qs], start=True, stop=True),
                                r=[kkey, ("qT", cb, blk)], w=[("ps", bank)])
                        for hf in range(2):
                            bank = 2 * kb + hf
                            dst = pt[par][:, kb].rearrange("p (i two) q -> p i two q", two=2)[:, :, hf, :]
                            self.act((dst, ("pt", par, kb)), (self.ps[bank][:, 0:256].rearrange("p (i q) -> p i q", q=128), ("ps", bank)),
                                     AF.Exp, scale=0.125)
                    bm = band0 if blk == 0 else band
                    self.tt((pt[par][:], [("pt", par, 0), ("pt", par, 1)]), (pt[par][:], [("pt", par, 0), ("pt", par, 1)]),
                            (bm[:].to_broadcast([128, 2, 4, 128]), "band0" if blk == 0 else "band"), MUL)
                    for kb in range(2):
                        if kb == 1:
                            vsrc, vkey = v2[:, blk, g, :], ("v2", g, blk // 4)
                        elif blk > 0:
                            vsrc, vkey = v2[:, blk - 1, g, :], ("v2", g, (blk - 1) // 4)
                        else:
                            vsrc, vkey = hv[:, g, :], "hv"
                        rhs = pt[par][:, kb].rearrange("p h q -> p (h q)")
                        self.M(lambda e, vsrc=vsrc, rhs=rhs, kb=kb, bO=bO: e.matmul(self.ps[bO][:], lhsT=vsrc, rhs=rhs, start=(kb == 0), stop=(kb == 1)),
                               r=[vkey, ("pt", par, 0), ("pt", par, 1)], w=[("ps", bO)])
                        self.M(lambda e, rhs=rhs, kb=kb, bD=bD: e.matmul(self.ps[bD][:], lhsT=self.ones_bf[:], rhs=rhs, start=(kb == 0), stop=(kb == 1)),
                               r=["ones", ("pt", par, 0), ("pt", par, 1)], w=[("ps", bD)])
                    dk_ = ("den", par)
                    self.tt((den[par][:], dk_), (self.ps[bD][:].rearrange("p (h q) -> p h q", q=128), ("ps", bD)),
                            (esk[:, 4 * g:4 * g + 4, :].to_broadcast([128, 4, 128]), "esk"), ADD)
                    self.V(lambda e, par=par: e.reciprocal(out=den[par][:], in_=den[par][:]), r=[dk_], w=[dk_])
                    po = self.ps[bO][:].rearrange("p (c two q) -> p c two q", two=2, q=128)
                    dv = den[par][:].rearrange("p (c two) q -> p c two q", two=2)
                    for hf in range(2):
                        rows = slice(64 * hf, 64 * hf + 64)
                        self.tt((qT[rows, 2 * g:2 * g + 2, qs], [("qT", 2 * g, blk), ("qT", 2 * g + 1, blk)]),
                                (po[rows, :, hf, :], ("ps", bO)), (dv[rows, :, hf, :], dk_), MUL)
            self.trunc(53)
            P.dma("sync", self.yA.rearrange("(c p) t -> p c t", p=128), qT[:],
                  reads=[("qT", c, b_) for c in range(8) for b_ in range(16)], writes=["yA"])
            P.barrier()

    def stage_merge(self):
        P = self.P
        MUL, ADD = ALU.mult, ALU.add
        d_wmg = self.din("w_mg", [D, 3 * D])
        d_wb = [self.din(n, [1024, D]) for n in ("w_branch_ssm", "w_branch_att", "w_branch_gla")]
        ysrc = [self.din_bf(n, [1024, T]) for n in ("yS", "yA", "yG")]
        self.mixT = self.dout_bf("mixT", [D, T])
        with contextlib.ExitStack() as st:
            sb = lambda n, shp, dt=F32: self.sb(st, n, shp, dt)
            yh = [sb("yh%d" % b, [128, 8, 1024], BF16) for b in range(3)]
            wmg = [sb("wmg%d" % i, [128, 16, 128], BF16) for i in range(2)]
            wbr = [sb("wbr%d" % i, [128, 8, 128], BF16) for i in range(2)]
            gt = [sb("gt%d" % i, [128, 512]) for i in range(2)]
            mt = sb("mt", [128, 512])
            acc = [sb("acc%d" % i, [128, 512]) for i in range(2)]
            mixh = sb("mixh", [128, 16, 1024], BF16)
            cnt = 0
            wc = 0
            for th in range(2):
                t0 = th * 1024
                for b in range(3):
                    P.dma("sync", yh[b][:], ysrc[b].rearrange("(c p) t -> p c t", p=128)[:, :, t0:t0 + 1024], writes=[("yh", b)])
                for c in range(16):
                    for b in range(3):
                        i = wc % 2
                        wc += 1
                        self.load_w(wmg[i], ("wmg", i), d_wmg, b * D + c * 128, 128, 16)
                        self.load_w(wbr[i], ("wbr", i), d_wb[b], c * 128, 128, 8)
                        for tg2 in range(2):
                            tg = th * 2 + tg2
                            bg, bb = cnt % 2, 2 + cnt % 2
                            gi = cnt % 2
                            cnt += 1
                            self.mm(self.ps[bg][:], ("ps", bg), wmg[i], ("wmg", i), 0, 128, self.hT_act(tg), 16)
                            self.mm(self.ps[bb][:], ("ps", bb), wbr[i], ("wbr", i), 0, 128,
                                    lambda k, b=b, tg2=tg2: (yh[b][:, k, tg2 * 512:(tg2 + 1) * 512], ("yh", b)), 8)
                            self.act((gt[gi][:], ("gt", gi)), (self.ps[bg][:], ("ps", bg)), AF.Sigmoid)
                            pb = (self.ps[bb][:], ("ps", bb))
                            a_ = (acc[tg2][:], ("acc", tg2))
                            if b == 0:
                                self.tt(a_, (gt[gi][:], ("gt", gi)), pb, MUL)
                            else:
                                self.tt((mt[:], "mt"), (gt[gi][:], ("gt", gi)), pb, MUL)
                                if b == 1:
                                    self.tt(a_, a_, (mt[:], "mt"), ADD, eng="gpsimd")
                                else:
                                    self.tt((mixh[:, c, tg2 * 512:(tg2 + 1) * 512], ("mixh", c)), a_, (mt[:], "mt"), ADD, eng="gpsimd")
                P.dma("sync", self.mixT.rearrange("(c p) t -> p c t", p=128)[:, :, t0:t0 + 1024], mixh[:],
                      reads=[("mixh", c) for c in range(16)], writes=[("mixT", th)])
            P.barrier()

    def stage_ffn(self):
        P = self.P
        MUL, ADD = ALU.mult, ALU.add
        d_wout = self.din("w_out", [D, D])
        d_n2 = self.din("norm2_g", [D])
        d_w1 = self.din("w_ff1", [D, 4 * D])
        d_w2 = self.din("w_ff2", [4 * D, D])
        if self.last:
            d_fn = self.din("final_norm_g", [D])
        d_xo = self.dout("xT_out", [D, T])
        if self.d_x is None:
            self.d_x = self.din("xT", [D, T])
        self.mixT = self.din_bf("mixT", [D, T])
        xv = self.d_x.rearrange("(k p) t -> p k t", p=128)
        xo = d_xo.rearrange("(k p) t -> p k t", p=128)
        mv = self.mixT.rearrange("(c p) t -> p c t", p=128)
        with contextlib.ExitStack() as st:
            sb = lambda n, shp, dt=F32: self.sb(st, n, shp, dt)
            xacc = sb("xacc", [128, 16, 1024])
            mh = sb("mh", [128, 16, 1024], BF16)
            wo = [sb("wo%d" % i, [128, 16, 128], BF16) for i in range(2)]
            w1 = [sb("w1_%d" % i, [128, 16, 512], BF16) for i in range(2)]
            w2 = [sb("w2_%d" % i, [128, 4, D], BF16) for i in range(2)]
            Ft = [sb("Ft%d" % i, [128, 4, 512], BF16) for i in range(2)]
            rl = [sb("rl%d" % i, [128, 512]) for i in range(2)]
            sq = sb("sq2", [128, 2, 512], BF16)
            rs = sb("rs2", [128, 512])
            n2g = self.load_vec(st, "n2g", d_n2, 16)
            if self.last:
                fng = self.load_vec(st, "fng", d_fn, 16)
            pc = 0
            for th in range(2):
                t0 = th * 1024
                P.dma("sync", xacc[:], xv[:, :, t0:t0 + 1024], writes=[("xacc", c, g2) for c in range(16) for g2 in range(2)])
                P.dma("sync", mh[:], mv[:, :, t0:t0 + 1024], writes=[("mh", 0), ("mh", 1)])
                for c in range(16):
                    i = c % 2
                    self.load_w(wo[i], ("wo", i), d_wout, c * 128, 128, 16)
                    for tg2 in range(2):
                        sl = slice(tg2 * 512, (tg2 + 1) * 512)
                        bank = pc % 4
                        pc += 1
                        self.mm(self.ps[bank][:], ("ps", bank), wo[i], ("wo", i), 0, 128,
                                lambda k, sl=sl, tg2=tg2: (mh[:, k, sl], ("mh", tg2)), 16)
                        xa = (xacc[:, c, sl], ("xacc", c, tg2))
                        self.tt(xa, xa, (self.ps[bank][:], ("ps", bank)), ADD)
                for tg2 in range(2):
                    sl = slice(tg2 * 512, (tg2 + 1) * 512)
                    self.norm_tg(lambda kc, sl=sl: xacc[:, kc, sl], lambda kc, tg2=tg2: ("xacc", kc, tg2), n2g, "n2g",
                                 lambda kc, sl=sl: mh[:, kc, sl], lambda kc, tg2=tg2: ("mh", tg2), tg2, (sq, rs))
                for fg in range(16):
                    i = fg % 2
                    self.load_w(w1[i], ("w1", i), d_w1, fg * 512, 512, 16)
                    self.load_w(w2[i], ("w2", i), d_w2, 0, D, 4, r0=fg * 512)
                    for tg2 in range(2):
                        sl = slice(tg2 * 512, (tg2 + 1) * 512)
                        f = (fg * 2 + tg2) % 2
                        for j in range(4):
                            bank = pc % 4
                            ri = pc % 2
                            pc += 1
                            self.mm(self.ps[bank][:], ("ps", bank), w1[i], ("w1", i), j * 128, 128,
                                    lambda k, sl=sl, tg2=tg2: (mh[:, k, sl], ("mh", tg2)), 16)
                            self.act((rl[ri][:], ("rl", ri)), (self.ps[bank][:], ("ps", bank)), AF.Relu)
                            self.tt((Ft[f][:, j, :], ("Ft", f, j)), (rl[ri][:], ("rl", ri)), (rl[ri][:], ("rl", ri)), MUL, eng="gpsimd")
                        for c in range(16):
                            bank = 4 + pc % 4
                            pc += 1
                            for j in range(4):
                                self.M(lambda e, i=i, j=j, c=c, f=f, bank=bank: e.matmul(self.ps[bank][:], lhsT=w2[i][:, j, c * 128:(c + 1) * 128],
                                                                                       rhs=Ft[f][:, j, :], start=(j == 0), stop=(j == 3)),
                                       r=[("w2", i), ("Ft", f, j)], w=[("ps", bank)])
                            xa = (xacc[:, c, sl], ("xacc", c, tg2))
                            self.tt(xa, xa, (self.ps[bank][:], ("ps", bank)), ADD)
                if self.last:
                    for tg2 in range(2):
                        sl = slice(tg2 * 512, (tg2 + 1) * 512)
                        self.norm_tg(lambda kc, sl=sl: xacc[:, kc, sl], lambda kc, tg2=tg2: ("xacc", kc, tg2), fng, "fng",
                                     lambda kc, sl=sl: xacc[:, kc, sl], lambda kc, tg2=tg2: ("xacc", kc, tg2), tg2, (sq, rs))
                P.dma("sync", xo[:, :, t0:t0 + 1024], xacc[:], reads=[("xacc", c, g2) for c in range(16) for g2 in range(2)],
                      writes=[("xo", th)])
            P.barrier()


def build_program(mode, last=False, dbg=False):
    b = Builder("A" if mode == "A" else "B", last, dbg)
    P = b.P
    try:
        _build_body(b, mode, last)
    except StopBuild:
        print("build truncated")
    P.barrier()
    P.emit()
    return b


def _build_body(b, mode, last):
    P = b.P
    with contextlib.ExitStack() as st0:
        b.setup_consts(st0)
        if mode == "B12":
            d_cm = b.din("cmask", [128, NCORES])
            b.cmask = b.sb(st0, "cmaskt", [128, NCORES], F32)
            P.dma("sync", b.cmask[:], d_cm, writes=["cmask"])
        if mode in ("A", "B12"):
            d_wu = b.din("w_u", [D, 1024])
            with contextlib.ExitStack() as st1:
                uT = b.sb(st1, "uT", [128, 8, T], BF16)
                with contextlib.ExitStack() as st2:
                    b.stage_norm1(st2)
                    b.wts = [b.sb(st2, "wtA%d" % i, [128, 16, 512], BF16) for i in range(2)]
                    for half in range(2):
                        def cons(cb, tg, ps_ap, pk, half=half):
                            o = half * 4 + cb
                            b.cp((uT[:, o, tg * 512:(tg + 1) * 512], ("uT", o, tg)), (ps_ap, pk), eng="scalar")
                        b.proj(d_wu, half * 512, 512, 16, b.hT_act, cons, [4, 5, 6, 7])
                    P.barrier()
                b.dbg("dbg_uT", uT[:], [128, 8, T], BF16)
                import os
                if os.environ.get("TRUNC") == "1":
                    if not int(os.environ.get("NPAD", "0")):
                        raise StopBuild()
                    npad = int(os.environ.get("NPAD", "0"))
                    if npad:
                        with contextlib.ExitStack() as stp:
                            padt = b.sb(stp, "padt", [128, 64], F32)
                            padg = b.sb(stp, "padg", [128, 64], F32)
                            for i_ in range(npad):
                                for en_ in os.environ.get("PADENG", "vector,gpsimd").split(","):
                                    tt_ = padt if en_ == "vector" else padg
                                    P.op(en_, lambda e, tt_=tt_: e.memset(tt_[:], 1.0), writes=["pad" + en_])
                    P.barrier()
                    P.emit()
                    return b
                b.stage_s5(uT)
            b.trunc(4)
            with contextlib.ExitStack() as st3:
                b.stage_norm1(st3)
                b.stage_swa()
                b.trunc(5)
                b.stage_gla()
        elif mode == "B3":
            with contextlib.ExitStack() as st3:
                b.stage_norm1(st3)
                b.stage_merge()
        elif mode == "B4":
            b.stage_ffn()


_PROGS = {}
_SPLITS = [0, 1024, 2048, 2304, 2560, 3072, 3584, 4608, 4624, 5648, 11792]
_SEG = ["w_u", "w_aq", "w_ak", "w_av", "w_gq", "w_gk", "w_gv", "w_glr", "w_gout", "w_mg"]


def _prog(mode, last=False):
    k = (mode, last)
    if k not in _PROGS:
        _PROGS[k] = build_program(mode, last)
    return _PROGS[k]


def _layer_inputs(inp, l):
    d = {}
    w_in = inp["w_in"][l]
    for i, n in enumerate(_SEG):
        d[n] = np.ascontiguousarray(w_in[:, _SPLITS[i]:_SPLITS[i + 1]])
    for n in ("norm1_g", "ssm_lam_re", "ssm_lam_im", "ssm_log_step", "ssm_b_re", "ssm_b_im", "ssm_c_re", "ssm_c_im",
              "ssm_d", "ssm_w_glu", "ssm_b_glu", "att_sinks", "gla_w_gate", "gla_b_gate", "gla_norm_g",
              "w_branch_ssm", "w_branch_att", "w_branch_gla", "w_out", "norm2_g", "w_ff1", "w_ff2"):
        d[n] = np.ascontiguousarray(inp[n][l])
    d["final_norm_g"] = np.ascontiguousarray(inp["final_norm_g"])
    d["consts"] = host_consts()
    return d


def _maps(b, per_core):
    maps = []
    for m in per_core:
        d = {}
        for k in b.in_names:
            if k in b.bf_inputs:
                d[k] = np.ascontiguousarray(m[k]).astype(ml_dtypes.bfloat16)
            else:
                d[k] = np.ascontiguousarray(m[k], dtype=np.float32)
        maps.append(d)
    return maps


def _run(b, per_core):
    res = run_bass_kernel_spmd(b.nc, _maps(b, per_core), core_ids=list(range(len(per_core))))
    return res.results


def run_layer(xT, lw, ropes, last):
    n = len(xT)
    base = []
    for c in range(n):
        m = dict(lw)
        m["xT"] = xT[c]
        m["rope"] = ropes[c]
        base.append(m)
    ra = _run(_prog("A"), base)
    s5_fins = np.stack([ra[c]["s5_fin"] for c in range(n)] + [np.zeros((128, 64), np.float32)] * (NCORES - n))
    gla_fins = np.stack([ra[c]["gla_fin"] for c in range(n)] + [np.zeros((128, 1024), np.float32)] * (NCORES - n))
    gla_dts = np.stack([ra[c]["gla_dtot"] for c in range(n)] + [np.ones((128, 4), np.float32)] * (NCORES - n))
    for c in range(n):
        m = base[c]
        m["s5_fins"] = s5_fins
        m["gla_fins"] = gla_fins
        m["gla_dtots"] = gla_dts
        cm = np.zeros((128, NCORES), np.float32)
        cm[:, :c] = 1.0
        m["cmask"] = cm
        if c > 0:
            m["halo_k"] = ra[c - 1]["halo_ko"]
            m["halo_v"] = ra[c - 1]["halo_vo"]
        else:
            m["halo_k"] = np.zeros((128, 4, 128), np.float32)
            m["halo_v"] = np.zeros((128, 4, 128), np.float32)
        m["hasprev"] = np.full((128, 1), 1.0 if c > 0 else 0.0, np.float32)
    r12 = _run(_prog("B12"), base)
    for c in range(n):
        for k in ("yS", "yA", "yG"):
            base[c][k] = r12[c][k]
    r3 = _run(_prog("B3"), base)
    for c in range(n):
        base[c]["mixT"] = r3[c]["mixT"]
    r4 = _run(_prog("B4", last), base)
    return [r4[c]["xT_out"] for c in range(n)], dict(ra=ra, r12=r12, r3=r3)


def kernel(**inputs):
    inp = {k: np.asarray(v) for k, v in inputs.items()}
    x = inp["x"][0]
    xT = [np.ascontiguousarray(x[c * T:(c + 1) * T].T) for c in range(NCORES)]
    ropes = [rope_tables(c) for c in range(NCORES)]
    for l in range(2):
        lw = _layer_inputs(inp, l)
        xT, _ = run_layer(xT, lw, ropes, last=(l == 1))
    out = np.concatenate([np.ascontiguousarray(xT[c].T) for c in range(NCORES)], axis=0)
    return out[None].astype(np.float32)
```
